# Optimizing a Trainium2 kernel written in Bass

```python
import math
import jax, jax.numpy as jnp
from jax import lax
import numpy as np

D_MODEL = 1024
BATCH = 4
SEQ = 4096
DEPTH = 4
DEC_BATCH = 16
DEC_SEQ = 32
PAST_LEN = 4096

CHUNK = 64
Q_BLOCK = 128
N_MIXERS = 3
EPS = 1e-6
DIFF_HEADS = 8
DIFF_HEAD_DIM = 64
DIFF_QK_WIDTH = DIFF_HEADS * 2 * DIFF_HEAD_DIM
DIFF_SCALE = DIFF_HEAD_DIM ** -0.5
CONV_WIDTH = 31
FOX_HEADS = 16
FOX_HEAD_DIM = 64
FOX_WIDTH = FOX_HEADS * FOX_HEAD_DIM
FOX_SCALE = FOX_HEAD_DIM ** -0.5
FORGET_BIAS_INIT = 3.0
REL_BUCKETS = 32
REL_MAX_DISTANCE = 128
D_FF = 4 * D_MODEL

N_DIFF_LAYERS = len(range(0, DEPTH, N_MIXERS))
N_CONV_LAYERS = len(range(1, DEPTH, N_MIXERS))
N_FOX_LAYERS = len(range(2, DEPTH, N_MIXERS))

kernel_name = "hybrid_diffattn_conformer_fox_stream_step"


def _rmsnorm(x, g):
    xf = x.astype(jnp.float32)
    y = xf * lax.rsqrt(jnp.mean(xf * xf, axis=-1, keepdims=True) + EPS)
    return (y * g.astype(jnp.float32)).astype(x.dtype)


def _layernorm(x, g, b):
    xf = x.astype(jnp.float32)
    xc = xf - jnp.mean(xf, axis=-1, keepdims=True)
    y = xc * lax.rsqrt(jnp.mean(xc * xc, axis=-1, keepdims=True) + EPS)
    return (y * g.astype(jnp.float32) + b.astype(jnp.float32)).astype(x.dtype)


def _sq_relu_mlp(h, w1, w2):
    return jnp.square(jax.nn.relu(h @ w1)) @ w2


def _t5_bucket(rel):
    half = REL_BUCKETS // 2
    n = -rel
    offset = jnp.where(n < 0, half, 0)
    n = jnp.abs(n)
    max_exact = half // 2
    large = max_exact + (jnp.log(jnp.maximum(n, 1).astype(jnp.float32) / max_exact)
                         / math.log(REL_MAX_DISTANCE / max_exact)
                         * (half - max_exact)).astype(jnp.int32)
    large = jnp.minimum(large, half - 1)
    return offset + jnp.where(n < max_exact, n, large)


def _rel_bias(table, q_pos, k_pos):
    b = table.astype(jnp.float32)[_t5_bucket(k_pos[None, :] - q_pos[:, None])]
    return jnp.transpose(b, (2, 0, 1))


def _diff_qkv(h, w_in):
    B, T, _ = h.shape
    u = h @ w_in
    q = u[..., :DIFF_QK_WIDTH].reshape(B, T, DIFF_HEADS, 2, DIFF_HEAD_DIM)
    k = u[..., DIFF_QK_WIDTH:2 * DIFF_QK_WIDTH].reshape(B, T, DIFF_HEADS, 2, DIFF_HEAD_DIM)
    v = u[..., 2 * DIFF_QK_WIDTH:].reshape(B, T, DIFF_HEADS, 2 * DIFF_HEAD_DIM)
    return q, k, v


def _diff_attend(q, k, v, q_pos, k_pos, lam, rel_table):
    s = jnp.einsum('bqhmd,bkhmd->bhmqk', q, k, preferred_element_type=jnp.float32) * DIFF_SCALE
    s = s + _rel_bias(rel_table, q_pos, k_pos)[None, :, None]
    chunk_mask = (k_pos[None, :] // CHUNK) <= (q_pos[:, None] // CHUNK)
    p = jax.nn.softmax(jnp.where(chunk_mask, s, -jnp.inf), axis=-1)
    a = p[:, :, 0] - lam * p[:, :, 1]
    return jnp.einsum('bhqk,bkhe->bqhe', a.astype(v.dtype), v)


def _diff_out(o, subln_g, lam_init, w_out):
    B, T = o.shape[:2]
    o = _rmsnorm(o, subln_g) * (1.0 - lam_init)
    return o.reshape(B, T, DIFF_QK_WIDTH) @ w_out


def _fox_qkv(h, w_in, b_f):
    B, T, _ = h.shape
    u = h @ w_in
    q = u[..., :FOX_WIDTH].reshape(B, T, FOX_HEADS, FOX_HEAD_DIM)
    k = u[..., FOX_WIDTH:2 * FOX_WIDTH].reshape(B, T, FOX_HEADS, FOX_HEAD_DIM)
    v = u[..., 2 * FOX_WIDTH:3 * FOX_WIDTH].reshape(B, T, FOX_HEADS, FOX_HEAD_DIM)
    logf = jax.nn.log_sigmoid((u[..., 3 * FOX_WIDTH:] + b_f).astype(jnp.float32))
    return q, k, v, logf


def _fox_attend(q, k, v, cq, ck, q_pos, k_pos):
    s = jnp.einsum('bqhd,bkhd->bhqk', q, k, preferred_element_type=jnp.float32) * FOX_SCALE
    s = s + jnp.transpose(cq, (0, 2, 1))[:, :, :, None] - jnp.transpose(ck, (0, 2, 1))[:, :, None, :]
    p = jax.nn.softmax(jnp.where(k_pos[None, :] <= q_pos[:, None], s, -jnp.inf), axis=-1)
    return jnp.einsum('bhqk,bkhd->bqhd', p.astype(v.dtype), v)


def _conv_glu(h, w_pw1, b_pw1):
    a, g = jnp.split(h @ w_pw1 + b_pw1, 2, axis=-1)
    return a * jax.nn.sigmoid(g)


def _conv_tail(u_padded, w_dw, b_dw, ln_g, ln_b, w_pw2, b_pw2):
    y = lax.conv_general_dilated(u_padded, w_dw[:, None, :], window_strides=(1,), padding='VALID',
                                 dimension_numbers=('NWC', 'WIO', 'NWC'),
                                 feature_group_count=D_MODEL) + b_dw
    y = jax.nn.silu(_layernorm(y, ln_g, ln_b))
    return y @ w_pw2 + b_pw2


def setup_inputs(seed: int = 0) -> dict:
    key = jax.random.key(seed)
    ks = iter(jax.random.split(key, 40))

    def nrm(shape, scale=1.0):
        return scale * jax.random.normal(next(ks), shape, jnp.float32)

    D = D_MODEL
    return {
        "x_prompt": nrm((BATCH, SEQ, D)),
        "x_sample": nrm((DEC_BATCH, DEC_SEQ, D)),
        "cache_diff_k": nrm((N_DIFF_LAYERS, DEC_BATCH, PAST_LEN, DIFF_HEADS, 2, DIFF_HEAD_DIM)),
        "cache_diff_v": nrm((N_DIFF_LAYERS, DEC_BATCH, PAST_LEN, DIFF_HEADS, 2 * DIFF_HEAD_DIM)),
        "state_conv": nrm((N_CONV_LAYERS, DEC_BATCH, CONV_WIDTH - 1, D), 0.5),
        "cache_fox_k": nrm((N_FOX_LAYERS, DEC_BATCH, PAST_LEN, FOX_HEADS, FOX_HEAD_DIM)),
        "cache_fox_v": nrm((N_FOX_LAYERS, DEC_BATCH, PAST_LEN, FOX_HEADS, FOX_HEAD_DIM)),
        "cache_fox_logf": jax.nn.log_sigmoid(FORGET_BIAS_INIT + nrm((N_FOX_LAYERS, DEC_BATCH, PAST_LEN, FOX_HEADS))),
        "rel_bias": nrm((REL_BUCKETS, DIFF_HEADS), 0.5),
        "norm_g": 1.0 + nrm((DEPTH, 2, D), 0.02),
        "final_g": 1.0 + nrm((D,), 0.02),
        "diff_w_in": nrm((N_DIFF_LAYERS, D, 3 * DIFF_QK_WIDTH), D ** -0.5),
        "diff_w_out": nrm((N_DIFF_LAYERS, DIFF_QK_WIDTH, D), DIFF_QK_WIDTH ** -0.5),
        "diff_lq1": nrm((N_DIFF_LAYERS, DIFF_HEAD_DIM), 0.1),
        "diff_lk1": nrm((N_DIFF_LAYERS, DIFF_HEAD_DIM), 0.1),
        "diff_lq2": nrm((N_DIFF_LAYERS, DIFF_HEAD_DIM), 0.1),
        "diff_lk2": nrm((N_DIFF_LAYERS, DIFF_HEAD_DIM), 0.1),
        "diff_subln_g": 1.0 + nrm((N_DIFF_LAYERS, 2 * DIFF_HEAD_DIM), 0.02),
        "conv_w_pw1": nrm((N_CONV_LAYERS, D, 2 * D), D ** -0.5),
        "conv_b_pw1": nrm((N_CONV_LAYERS, 2 * D), 0.02),
        "conv_w_dw": nrm((N_CONV_LAYERS, CONV_WIDTH, D), CONV_WIDTH ** -0.5),
        "conv_b_dw": nrm((N_CONV_LAYERS, D), 0.02),
        "conv_ln_g": 1.0 + nrm((N_CONV_LAYERS, D), 0.02),
        "conv_ln_b": nrm((N_CONV_LAYERS, D), 0.02),
        "conv_w_pw2": nrm((N_CONV_LAYERS, D, D), D ** -0.5),
        "conv_b_pw2": nrm((N_CONV_LAYERS, D), 0.02),
        "fox_w_in": nrm((N_FOX_LAYERS, D, 3 * FOX_WIDTH + FOX_HEADS), D ** -0.5),
        "fox_b_f": FORGET_BIAS_INIT + nrm((N_FOX_LAYERS, FOX_HEADS), 0.1),
        "fox_w_out": nrm((N_FOX_LAYERS, FOX_WIDTH, D), FOX_WIDTH ** -0.5),
        "mlp_w1": nrm((DEPTH, D, D_FF), D ** -0.5),
        "mlp_w2": nrm((DEPTH, D_FF, D), D_FF ** -0.5),
    }


def reference(x_prompt, x_sample, cache_diff_k, cache_diff_v, state_conv, cache_fox_k, cache_fox_v,
              cache_fox_logf, rel_bias, norm_g, final_g, diff_w_in, diff_w_out, diff_lq1, diff_lk1,
              diff_lq2, diff_lk2, diff_subln_g, conv_w_pw1, conv_b_pw1, conv_w_dw, conv_b_dw,
              conv_ln_g, conv_ln_b, conv_w_pw2, conv_b_pw2, fox_w_in, fox_b_f, fox_w_out,
              mlp_w1, mlp_w2):
    B, S, _ = x_prompt.shape
    T = x_sample.shape[1]
    P = cache_diff_k.shape[2]
    pos_p = jnp.arange(S)
    q_pos_s = P + jnp.arange(T)
    k_pos_s = jnp.arange(P + T)
    q_starts = jnp.arange(0, S, Q_BLOCK)
    q_offsets = jnp.arange(Q_BLOCK)

    def sweep(block_fn):
        o = lax.map(block_fn, q_starts)
        return jnp.moveaxis(o, 0, 1).reshape((B, S) + o.shape[3:])

    xp, xs = x_prompt, x_sample
    dk_p, dv_p, dk_s, dv_s = [], [], [], []
    cv_p, cv_s = [], []
    fk_p, fv_p, fl_p, fk_s, fv_s, fl_s = [], [], [], [], [], []

    for i in range(DEPTH):
        kind, j = i % N_MIXERS, i // N_MIXERS
        hp = _rmsnorm(xp, norm_g[i, 0])
        hs = _rmsnorm(xs, norm_g[i, 0])
        if kind == 0:
            lam_init = 0.8 - 0.6 * math.exp(-0.3 * i)
            lam = (jnp.exp(jnp.sum(diff_lq1[j].astype(jnp.float32) * diff_lk1[j].astype(jnp.float32)))
                   - jnp.exp(jnp.sum(diff_lq2[j].astype(jnp.float32) * diff_lk2[j].astype(jnp.float32)))
                   + lam_init)
            qp, kp, vp = _diff_qkv(hp, diff_w_in[j])

            def diff_block(qs, qp=qp, kp=kp, vp=vp, lam=lam):
                qb = lax.dynamic_slice_in_dim(qp, qs, Q_BLOCK, axis=1)
                return _diff_attend(qb, kp, vp, qs + q_offsets, pos_p, lam, rel_bias)

            op = sweep(diff_block)
            qn, kn, vn = _diff_qkv(hs, diff_w_in[j])
            k_all = jnp.concatenate([cache_diff_k[j], kn], axis=1)
            v_all = jnp.concatenate([cache_diff_v[j], vn], axis=1)
            on = _diff_attend(qn, k_all, v_all, q_pos_s, k_pos_s, lam, rel_bias)
            mp = _diff_out(op, diff_subln_g[j], lam_init, diff_w_out[j])
            ms = _diff_out(on, diff_subln_g[j], lam_init, diff_w_out[j])
            dk_p.append(kp); dv_p.append(vp); dk_s.append(kn); dv_s.append(vn)
        elif kind == 1:
            up = _conv_glu(hp, conv_w_pw1[j], conv_b_pw1[j])
            un = _conv_glu(hs, conv_w_pw1[j], conv_b_pw1[j])
            up_pad = jnp.pad(up, ((0, 0), (CONV_WIDTH - 1, 0), (0, 0)))
            un_all = jnp.concatenate([state_conv[j], un], axis=1)
            mp = _conv_tail(up_pad, conv_w_dw[j], conv_b_dw[j], conv_ln_g[j], conv_ln_b[j],
                            conv_w_pw2[j], conv_b_pw2[j])
            ms = _conv_tail(un_all, conv_w_dw[j], conv_b_dw[j], conv_ln_g[j], conv_ln_b[j],
                            conv_w_pw2[j], conv_b_pw2[j])
            cv_p.append(up[:, S - (CONV_WIDTH - 1):]); cv_s.append(un_all[:, T:])
        else:
            qp, kp, vp, lfp = _fox_qkv(hp, fox_w_in[j], fox_b_f[j])
            cp = jnp.cumsum(lfp, axis=1)

            def fox_block(qs, qp=qp, kp=kp, vp=vp, cp=cp):
                qb = lax.dynamic_slice_in_dim(qp, qs, Q_BLOCK, axis=1)
                cb = lax.dynamic_slice_in_dim(cp, qs, Q_BLOCK, axis=1)
                return _fox_attend(qb, kp, vp, cb, cp, qs + q_offsets, pos_p)

            op = sweep(fox_block)
            qn, kn, vn, lfn = _fox_qkv(hs, fox_w_in[j], fox_b_f[j])
            c_all = jnp.cumsum(jnp.concatenate([cache_fox_logf[j].astype(jnp.float32), lfn], axis=1), axis=1)
            k_all = jnp.concatenate([cache_fox_k[j], kn], axis=1)
            v_all = jnp.concatenate([cache_fox_v[j], vn], axis=1)
            on = _fox_attend(qn, k_all, v_all, c_all[:, P:], c_all, q_pos_s, k_pos_s)
            mp = op.reshape(B, S, FOX_WIDTH) @ fox_w_out[j]
            ms = on.reshape(on.shape[0], T, FOX_WIDTH) @ fox_w_out[j]
            fk_p.append(kp); fv_p.append(vp); fl_p.append(lfp)
            fk_s.append(kn); fv_s.append(vn); fl_s.append(lfn)
        xp = xp + mp
        xs = xs + ms
        xp = xp + _sq_relu_mlp(_rmsnorm(xp, norm_g[i, 1]), mlp_w1[i], mlp_w2[i])
        xs = xs + _sq_relu_mlp(_rmsnorm(xs, norm_g[i, 1]), mlp_w1[i], mlp_w2[i])

    y_prompt = _rmsnorm(xp, final_g)
    y_sample = _rmsnorm(xs, final_g)
    new_diff_k_p = jnp.stack(dk_p)
    new_diff_v_p = jnp.stack(dv_p)
    new_conv_p = jnp.stack(cv_p)
    new_fox_k_p = jnp.stack(fk_p)
    new_fox_v_p = jnp.stack(fv_p)
    new_fox_logf_p = jnp.stack(fl_p)
    new_diff_k_s = jnp.stack(dk_s)
    new_diff_v_s = jnp.stack(dv_s)
    new_conv_s = jnp.stack(cv_s)
    new_fox_k_s = jnp.stack(fk_s)
    new_fox_v_s = jnp.stack(fv_s)
    new_fox_logf_s = jnp.stack(fl_s)
    return (y_prompt, y_sample, new_diff_k_p, new_diff_v_p, new_conv_p, new_fox_k_p, new_fox_v_p,
            new_fox_logf_p, new_diff_k_s, new_diff_v_s, new_conv_s, new_fox_k_s, new_fox_v_s,
            new_fox_logf_s)
```

```python
import math
import numpy as np
import ml_dtypes
import concourse.bass as bass
import concourse.mybir as mybir
from concourse.bass_utils import run_bass_kernel_spmd

F32 = mybir.dt.float32
BF16 = mybir.dt.bfloat16
AF = mybir.ActivationFunctionType
ALU = mybir.AluOpType

P = 128
NP_ = 2048
NS_ = 64
NTOK = NP_ + NS_
TT = [(0, 512), (512, 512), (1024, 512), (1536, 512), (2048, 64)]
EPS = 1e-6
RZ = 1664
RZ2 = 384
SW = 1536
NEG = -30000.0
DEPTH = 4

ARENA = 211968
X0, C0, W0, A0, B0, S0 = 0, 67584, 75776, 108544, 142336, 179200
SSZ = ARENA - S0


class Buf:
    __slots__ = ("name", "w", "r", "pr", "excl")

    def __init__(self, name, excl=False):
        self.name = name
        self.excl = excl
        self.w = {}
        self.r = {}
        self.pr = {}


class Op:
    __slots__ = ("eng", "kind", "args", "kw", "deps", "need_inc", "ord", "sem", "val", "dma", "inc", "key")


class SemCounter:
    def __init__(self, nc, stack, name, limit=30000):
        self.nc, self.stack, self.name, self.limit = nc, stack, name, limit
        self.sems = []
        self.finals = []
        self.val = 0
        self._new()

    def _new(self):
        if self.sems:
            self.finals.append(self.val)
        s = self.stack.enter_context(self.nc.semaphore("%s_%d" % (self.name, len(self.sems))))
        self.sems.append(s)
        self.val = 0

    def bump(self, inc):
        if self.val + inc > self.limit:
            self._new()
        self.val += inc
        return self.sems[-1], self.val


def _merge(d, op):
    o = d.get(op.key)
    if o is None or o.ord < op.ord:
        d[op.key] = op


class Sched:
    ENGS = ["pe", "act", "dve", "pool", "sp"]

    def __init__(self, nc, stack):
        self.nc, self.stack = nc, stack
        self.ops = {e: [] for e in self.ENGS}
        self.streams = {}
        self.gord = 0
        self.bar = {e: None for e in self.ENGS}
        self.all_dma_last = {}

    def add(self, eng, kind, args, kw, R=(), W=(), PW=(), stream=None, inc=16):
        op = Op()
        op.eng, op.kind, op.args, op.kw = eng, kind, args, kw
        op.need_inc = False
        op.dma = stream is not None
        op.inc = inc
        op.sem = None
        op.val = 0
        if op.dma:
            sc = self.streams.get(stream)
            if sc is None:
                sc = SemCounter(self.nc, self.stack, "d" + str(len(self.streams)))
                self.streams[stream] = sc
            op.sem, op.val = sc.bump(inc)
            op.key = ("d", id(op.sem))
            op.ord = op.val
            self.all_dma_last[op.key] = op
        else:
            op.key = eng
            op.ord = len(self.ops[eng])
        deps = {}
        for b in R:
            for o in b.w.values():
                _merge(deps, o)
            if b.excl:
                for o in b.r.values():
                    if o.key != op.key:
                        _merge(deps, o)
        for b in W:
            for d in (b.w, b.r, b.pr):
                for o in d.values():
                    _merge(deps, o)
        for b in PW:
            if b.r:
                b.pr = b.r
                b.r = {}
                b.w = {}
            for o in b.pr.values():
                _merge(deps, o)
        if self.bar[eng] is not None:
            for o in self.bar[eng].values():
                _merge(deps, o)
            self.bar[eng] = None
        op.deps = deps
        for b in R:
            _merge(b.r, op)
        for b in W:
            b.w = {op.key: op}
            b.r = {}
            b.pr = {}
        for b in PW:
            _merge(b.w, op)
        self.ops[eng].append(op)
        return op

    def barrier(self):
        snap = {}
        for e in self.ENGS:
            if e != "sp" and self.ops[e]:
                o = self.ops[e][-1]
                if not o.dma:
                    snap[o.key] = o
                else:
                    for q in reversed(self.ops[e]):
                        if not q.dma:
                            snap[q.key] = q
                            break
        for k, o in self.all_dma_last.items():
            snap[k] = o
        for e in self.ENGS:
            d = dict(snap)
            if self.bar[e] is not None:
                for o in self.bar[e].values():
                    _merge(d, o)
            self.bar[e] = d

    def finalize(self):
        for e in self.ENGS:
            for op in self.ops[e]:
                for d in op.deps.values():
                    if not d.dma and not (d.eng == op.eng == "pe"):
                        d.need_inc = True
        for e in self.ENGS:
            sc = SemCounter(self.nc, self.stack, "e_" + e)
            for op in self.ops[e]:
                if not op.dma and op.need_inc:
                    op.sem, op.val = sc.bump(1)

    def emit(self, eng_name, eng, final_wait=False):
        waited = {}
        for op in self.ops[eng_name]:
            for d in op.deps.values():
                if not d.dma and d.eng == eng_name == "pe":
                    continue
                k = id(d.sem)
                if waited.get(k, 0) < d.val:
                    eng.wait_ge(d.sem, d.val)
                    waited[k] = d.val
            ins = getattr(eng, op.kind)(*op.args, **op.kw)
            if op.dma:
                ins.then_inc(op.sem, op.inc)
            elif op.need_inc:
                ins.then_inc(op.sem, 1)
        if final_wait:
            for sc in self.streams.values():
                fin = sc.finals + [sc.val]
                for s, v in zip(sc.sems, fin):
                    if v:
                        eng.wait_ge(s, v)


class Builder:
    def __init__(self, nc, stack):
        self.nc = nc
        self.S = Sched(nc, stack)
        self.arena = stack.enter_context(nc.sbuf_tensor("arena", [P, ARENA // 4], F32))
        self.psum = stack.enter_context(nc.psum_tensor("psum", [P, 4096], F32))
        self.pb = [Buf("pb%d" % i, excl=True) for i in range(8)]
        self.sc_off = 0
        self.sc_names = 0
        self.dram = {}
        self.dbuf = {}

    def view(self, off, shape, dt):
        nb = int(np.prod(shape[1:])) * (4 if dt == F32 else 2)
        assert off % 4 == 0 and nb % 4 == 0
        ap = self.arena[0:shape[0], off // 4:(off + nb) // 4]
        if dt != F32:
            ap = ap.bitcast(dt)
        if len(shape) == 3:
            ap = ap.rearrange("p (a b) -> p a b", b=shape[2])
        elif len(shape) == 4:
            ap = ap.rearrange("p (a b c) -> p a b c", b=shape[2], c=shape[3])
        return ap

    def salloc(self, shape, dt, name=None):
        nb = int(np.prod(shape[1:])) * (4 if dt == F32 else 2)
        nb = (nb + 31) // 32 * 32
        off = S0 + self.sc_off
        self.sc_off += nb
        assert self.sc_off <= SSZ, ("scratch overflow", self.sc_off)
        self.sc_names += 1
        return self.view(off, shape, dt), Buf(name or ("s%d" % self.sc_names))

    def sreset(self):
        self.S.barrier()
        self.sc_off = 0

    def bank(self, i, rows=P, cols=512):
        return self.psum[0:rows, 512 * i:512 * i + cols]

    def mm(self, out, lhsT, rhs, start, stop, R, W):
        return self.S.add("pe", "matmul", (out,), dict(lhsT=lhsT, rhs=rhs, start=start, stop=stop), R=R, W=W)

    def act(self, out, in_, func, R, W=(), PW=(), bias=None, scale=None):
        kw = dict(out=out, in_=in_, func=func)
        if bias is not None:
            kw["bias"] = bias
        if scale is not None:
            kw["scale"] = scale
        return self.S.add("act", "activation", (), kw, R=R, W=W, PW=PW)

    def ts(self, eng, out, in0, s1, s2, op0, op1, R, W=(), PW=()):
        kw = dict(out=out, in0=in0, scalar1=s1, scalar2=s2, op0=op0)
        if op1 is not None:
            kw["op1"] = op1
        return self.S.add(eng, "tensor_scalar", (), kw, R=R, W=W, PW=PW)

    def tt(self, eng, out, in0, in1, op, R, W=(), PW=()):
        return self.S.add(eng, "tensor_tensor", (), dict(out=out, in0=in0, in1=in1, op=op), R=R, W=W, PW=PW)

    def stt(self, eng, out, in0, scalar, in1, op0, op1, R, W=(), PW=()):
        return self.S.add(eng, "scalar_tensor_tensor", (),
                          dict(out=out, in0=in0, scalar=scalar, in1=in1, op0=op0, op1=op1), R=R, W=W, PW=PW)

    def cp(self, eng, out, in_, R, W=(), PW=()):
        if eng == "act":
            return self.S.add("act", "copy", (), dict(out=out, in_=in_), R=R, W=W, PW=PW)
        return self.S.add(eng, "tensor_copy", (), dict(out=out, in_=in_), R=R, W=W, PW=PW)

    def recip(self, out, in_, R, W=(), PW=()):
        return self.S.add("dve", "reciprocal", (), dict(out=out, in_=in_), R=R, W=W, PW=PW)

    def memset(self, eng, ap, val, W=(), PW=()):
        return self.S.add(eng, "memset", (ap, val), {}, W=W, PW=PW)

    def dma(self, q, out, in_, stream, R=(), W=(), PW=()):
        return self.S.add(q, "dma_start", (), dict(out=out, in_=in_), R=R, W=W, PW=PW, stream=stream)


def lam_init_of(i):
    return 0.8 - 0.6 * math.exp(-0.3 * i)


NPP = 376
PP_NORM, PP_FINAL, PP_SUBLN, PP_BPW1, PP_WDW, PP_BDW, PP_LNG, PP_LNB, PP_BPW2 = 0, 64, 72, 74, 90, 338, 346, 354, 362
NROW = 512 + 16 + 8


class StopBuild(Exception):
    pass


class Prog(Builder):
    def stop(self, name):
        import os
        if os.environ.get("KSTOP", "") == name:
            raise StopBuild(name)

    def declare(self):
        nc = self.nc
        D = {}

        def inp(name, shape):
            D[name] = nc.dram_tensor(name, list(shape), F32, kind="ExternalInput")

        def outp(name, shape):
            D[name] = nc.dram_tensor(name, list(shape), F32, kind="ExternalOutput")

        def scr(name, shape, dt):
            D[name] = nc.dram_tensor(name, list(shape), dt)

        self.in_shapes = {
            "xT": (1024, NTOK), "ckT_d": (2, 2, 1024, 4096), "cv_d": (2, 2, 4096, 1024),
            "ckT_f": (2, 1024, 4096), "cv_f": (2, 4096, 1024), "clf": (2, 4096, 16), "scT": (2, 1024, 30),
            "diff_w_in": (2, 1024, 3072), "diff_w_out": (2, 1024, 1024), "conv_w_pw1": (1024, 2048), "conv_w_pw2": (1024, 1024),
            "fox_w_in": (1024, 3088), "fox_w_out": (1024, 1024), "mlp_w1": (4, 1024, 4096), "mlp_w2": (4, 4096, 1024),
            "pp": (128, NPP), "rowp": (1, NROW), "rel_bias": (32, 8), "oh1": (32, RZ), "oh2": (32, RZ2),
            "mstrip_d": (128, SW), "mstrip_f": (128, SW), "selp": (1, 16), "tri": (128, 128), "ident": (128, 128),
            "sel127": (128, 128), "masknew": (32, 512)}
        self._inp = inp
        outp("yT", (1024, NTOK)); outp("dkT", (2, 1024, NTOK)); outp("dv", (2, NTOK, 1024))
        outp("cvT", (1024, 94)); outp("fkT", (1024, NTOK)); outp("fv", (NTOK, 1024)); outp("flf", (NTOK, 16))
        for ab in "ab":
            scr("kx%s_in" % ab, (512, 2048), BF16); scr("kx%s_out" % ab, (1024, 2048), BF16)
            scr("vx%s_in" % ab, (1024, 1024), BF16); scr("vx%s_out" % ab, (2048, 1024), BF16)
        scr("lfx_in", (2048, 16), F32); scr("lfx_out", (4096, 16), F32)
        scr("cx_in", (4096, 32), BF16); scr("cx_out", (8192, 32), BF16)
        scr("zrep", (8, 128, RZ), F32); scr("zrep2", (8, 128, RZ2), F32)
        scr("strips", (8, 128, SW), BF16)
        self.D = D
        self.DB = {k: Buf("D_" + k) for k in D}

    def dt_(self, name):
        if name not in self.D:
            self._inp(name, self.in_shapes[name])
        return self.D[name]

    def d(self, name):
        return self.dt_(name).ap()

    def prologue(self):
        S = self.S
        o = C0

        def c(shape, dt):
            nonlocal o
            nb = int(np.prod(shape[1:])) * (4 if dt == F32 else 2)
            nb = (nb + 31) // 32 * 32
            v = self.view(o, shape, dt)
            o += nb
            return v
        self.ident_bf = c([P, 128], BF16); self.ones_bf = c([P, 128], BF16); self.zeros_bf = c([P, 128], BF16)
        self.ones_f = c([P, 128], F32); self.tri_f = c([P, 128], F32); self.sel127 = c([P, 128], F32)
        self.pp = c([P, NPP], F32); self.bfb = c([P, 16], F32); self.selb = c([P, 16], F32)
        self.neglam = c([P, 2], F32); self.gsc = c([P, 2], F32)
        self.zero_b = c([P, 1], F32); self.eps_b = c([P, 1], F32)
        self.negbg = c([P, 8], F32); self.neglnb = c([P, 8], F32)
        self.Bs31 = c([P, 512], BF16); self.BsN = c([32, 512], BF16); self.mnew = c([32, 512], BF16)
        self.wlf = c([P, 8, 16], BF16)
        assert o <= W0, o
        self.cb = Buf("consts")
        cb = self.cb
        self.x = self.view(X0, [P, 8, NTOK], F32)
        self.xb = [Buf("x%d" % t) for t in range(5)]
        self.bufA = self.view(A0, [P, 8, NTOK], BF16)
        self.ab = [Buf("a%d" % t) for t in range(5)]
        self.wslot = [self.view(W0 + 16384 * k, [P, 8, 1024], BF16) for k in range(2)]
        self.wb = [Buf("w0"), Buf("w1")]

        for cch in range(8):
            self.dma("sp", self.x[:, cch, :], self.d("xT")[cch * 128:(cch + 1) * 128, :], "xload", PW=self.xb)
        self.dma("pool", self.ident_bf, self.d("ident"), "cst0", PW=[cb])
        self.dma("sp", self.tri_f, self.d("tri"), "cst1", PW=[cb])
        self.dma("sp", self.sel127, self.d("sel127"), "cst1", PW=[cb])
        self.dma("sp", self.pp, self.d("pp"), "cst1", PW=[cb])
        self.dma("pool", self.mnew, self.d("masknew"), "cst0", PW=[cb])
        rowp_t = self.dt_("rowp")
        self.dma("sp", self.bfb, bass.AP(tensor=rowp_t, offset=512, ap=[[0, 128], [1, 16]]), "cst1", PW=[cb])
        self.dma("sp", self.selb, bass.AP(tensor=self.dt_("selp"), offset=0, ap=[[0, 128], [1, 16]]), "cst1", PW=[cb])
        self.dma("pool", self.wlf, self.d("fox_w_in")[:, 3072:3088].rearrange("(kc p) n -> p kc n", p=128), "cst0", PW=[cb])
        self.memset("dve", self.ones_bf, 1.0, PW=[cb])
        self.memset("dve", self.zeros_bf, 0.0, PW=[cb])
        self.memset("dve", self.ones_f, 1.0, PW=[cb])
        self.memset("dve", self.zero_b, 0.0, PW=[cb])
        self.memset("dve", self.eps_b, EPS, PW=[cb])
        self.ts("dve", self.negbg, self.pp[:, PP_BPW1 + 8:PP_BPW1 + 16], -1.0, None, ALU.mult, None, R=[cb], PW=[cb])
        self.ts("dve", self.neglnb, self.pp[:, PP_LNB:PP_LNB + 8], -1.0, None, ALU.mult, None, R=[cb], PW=[cb])
        rowb, rowbB = self.salloc([P, 512], F32, "rowb")
        self.dma("sp", rowb, bass.AP(tensor=rowp_t, offset=0, ap=[[0, 128], [1, 512]]), "rowb", W=[rowbB])
        pr, prB = self.salloc([P, 64], F32, "pr")
        sm, smB = self.salloc([P, 4], F32, "sm")
        for j, li in enumerate((0, 3)):
            for k in range(2):
                a = rowb[:, j * 256 + k * 128:j * 256 + k * 128 + 64]
                b_ = rowb[:, j * 256 + k * 128 + 64:j * 256 + k * 128 + 128]
                self.tt("dve", pr, a, b_, ALU.mult, R=[rowbB], W=[prB])
                S.add("dve", "reduce_sum", (), dict(out=sm[:, 2 * j + k:2 * j + k + 1], in_=pr, axis=mybir.AxisListType.X),
                      R=[prB], PW=[smB])
        ex, exB = self.salloc([P, 4], F32, "ex")
        self.act(ex, sm, AF.Exp, R=[smB, cb], W=[exB], bias=self.zero_b, scale=1.0)
        for j, li in enumerate((0, 3)):
            self.tt("dve", self.neglam[:, j:j + 1], ex[:, 2 * j + 1:2 * j + 2], ex[:, 2 * j:2 * j + 1], ALU.subtract, R=[exB], PW=[cb])
            self.ts("dve", self.neglam[:, j:j + 1], self.neglam[:, j:j + 1], -lam_init_of(li), None, ALU.add, None, R=[cb], PW=[cb])
            self.ts("dve", self.gsc[:, j:j + 1], self.pp[:, PP_SUBLN + j:PP_SUBLN + j + 1], 1.0 - lam_init_of(li), None,
                    ALU.mult, None, R=[cb], PW=[cb])
        tab, tabB = self.salloc([32, 8], F32, "tab")
        t15, t15B = self.salloc([32, 8], F32, "t15")
        self.dma("sp", tab, self.d("rel_bias"), "tab", W=[tabB])
        self.dma("sp", t15, bass.AP(tensor=self.dt_("rel_bias"), offset=15 * 8, ap=[[0, 32], [1, 8]]), "t15", W=[t15B])
        Dm, DmB = self.salloc([32, 8], F32, "Dm")
        self.tt("dve", Dm, tab, t15, ALU.subtract, R=[tabB, t15B], W=[DmB])
        self.ts("dve", Dm, Dm, 8.0, None, ALU.mult, None, R=[DmB], W=[DmB])
        oh1, oh1B = self.salloc([32, RZ], F32, "oh1")
        oh2, oh2B = self.salloc([32, RZ2], F32, "oh2")
        self.dma("sp", oh1, self.d("oh1"), "oh1", W=[oh1B])
        self.dma("sp", oh2, self.d("oh2"), "oh2", W=[oh2B])
        msd, msdB = self.salloc([P, SW], BF16, "msd")
        self.dma("pool", msd, self.d("mstrip_d"), "msd", W=[msdB])
        dbc, dbcB = self.salloc([32, 128], F32, "dbc")
        zst, zstB = self.salloc([P, RZ], F32, "zst")
        z2st, z2stB = self.salloc([P, RZ2], F32, "z2st")
        sk, skB = self.salloc([P, SW], F32, "sk")
        stb, stbB = self.salloc([P, SW], BF16, "stb")
        zrep, zrep2, strips = self.D["zrep"], self.D["zrep2"], self.D["strips"]
        for h in range(8):
            self.ts("dve", dbc, self.ones_f[0:32, :], Dm[:, h:h + 1], None, ALU.mult, None, R=[DmB, cb], W=[dbcB])
            for k, (c0, n) in enumerate([(0, 512), (512, 512), (1024, 512), (1536, 128)]):
                self.mm(self.bank(k, P, n), dbc, oh1[:, c0:c0 + n], True, True, R=[dbcB, oh1B], W=[self.pb[k]])
                self.cp("act" if k % 2 else "dve", zst[:, c0:c0 + n], self.bank(k, P, n), R=[self.pb[k]], PW=[zstB])
            self.mm(self.bank(4, P, RZ2), dbc, oh2, True, True, R=[dbcB, oh2B], W=[self.pb[4]])
            self.cp("act", z2st, self.bank(4, P, RZ2), R=[self.pb[4]], W=[z2stB])
            self.dma("sp", zrep.ap()[h], zst, "zst", R=[zstB], PW=[self.DB["zrep"]])
            self.dma("sp", zrep2.ap()[h], z2st, "z2st", R=[z2stB], PW=[self.DB["zrep2"]])
            skew = bass.AP(tensor=zrep, offset=h * 128 * RZ + 127, ap=[[RZ - 1, 128], [1, SW]])
            self.dma("sp", sk, skew, "sk", R=[self.DB["zrep"]], W=[skB])
            self.tt("dve", stb, sk, msd, ALU.add, R=[skB, msdB], W=[stbB])
            self.dma("sp", strips.ap()[h], stb, "stb", R=[stbB], PW=[self.DB["strips"]])
        for m in range(2):
            s31 = bass.AP(tensor=zrep2, offset=255, ap=[[RZ2 - 1, 128], [128 * RZ2, 8], [1, 32]])
            sN = bass.AP(tensor=zrep2, offset=127, ap=[[RZ2 - 1, 32], [128 * RZ2, 8], [1, 32]])
            self.dma("pool", self.Bs31.rearrange("p (h m q) -> p h m q", m=2, q=32)[:, :, m, :], s31, "cst0", R=[self.DB["zrep2"]], PW=[cb])
            self.dma("pool", self.BsN.rearrange("p (h m q) -> p h m q", m=2, q=32)[:, :, m, :], sN, "cst0", R=[self.DB["zrep2"]], PW=[cb])
        self.sreset()

    def unit_src(self, name, rows, cols, idx=None):
        a = self.d(name)
        if idx is not None:
            a = a[idx]
        return a[rows[0]:rows[1], cols[0]:cols[1]].rearrange("(kc p) n -> p kc n", p=128)

    def load_unit(self, slot, src):
        for half in range(2):
            self.dma("pool", self.wslot[slot][:, 4 * half:4 * half + 4, :], src[:, 4 * half:4 * half + 4, :],
                     "w%d" % slot, PW=[self.wb[slot]] if half else (), W=() if half else [self.wb[slot]])

    def units_begin(self, srcs):
        self.useq = list(srcs)
        self.upos = 0
        self.uloaded = 0
        while self.uloaded < min(2, len(self.useq)):
            self.load_unit(self.uloaded % 2, self.useq[self.uloaded])
            self.uloaded += 1

    def unit_use(self):
        k = self.upos
        self.upos += 1
        return self.wslot[k % 2], self.wb[k % 2]

    def unit_done(self):
        if self.uloaded < len(self.useq):
            self.load_unit(self.uloaded % 2, self.useq[self.uloaded])
            self.uloaded += 1

    def rmsnorm(self, gcol, out_f32_dma=None):
        m0 = self.sc_off
        sqs = [self.salloc([P, 512], BF16, "sq") for _ in range(4)]
        lnv = [self.salloc([P, 512], F32, "lnv") for _ in range(2)]
        rs = [self.salloc([P, 512], F32, "rs") for _ in range(2)]
        stg = [self.salloc([P, 512], F32, "ystg%d" % k_) for k_ in range(3)] if out_f32_dma is not None else None
        k = 0
        for t, (t0, n) in enumerate(TT):
            bk = t % 2
            for c in range(8):
                sq, sqB = sqs[k % 4]
                k += 1
                xs = self.x[:, c, t0:t0 + n]
                self.tt("dve" if c % 2 == 0 else "pool", sq[:, :n], xs, xs, ALU.mult, R=[self.xb[t]], W=[sqB])
                self.mm(self.bank(bk, P, n), self.ones_bf, sq[:, :n], c == 0, c == 7, R=[sqB, self.cb], W=[self.pb[bk]])
            l, lB = lnv[t % 2]
            r, rB = rs[t % 2]
            self.act(l[:, :n], self.bank(bk, P, n), AF.Ln, R=[self.pb[bk], self.cb], W=[lB], bias=self.eps_b, scale=1.0 / 1024)
            self.act(r[:, :n], l[:, :n], AF.Exp, R=[lB, self.cb], W=[rB], bias=self.zero_b, scale=-0.5)
            for c in range(8):
                g = self.pp[:, gcol + c:gcol + c + 1]
                if out_f32_dma is None:
                    self.stt("dve", self.bufA[:, c, t0:t0 + n], self.x[:, c, t0:t0 + n], g, r[:, :n], ALU.mult, ALU.mult,
                             R=[self.xb[t], rB, self.cb], PW=[self.ab[t]])
                else:
                    sg, sgB = stg[(t * 8 + c) % 3]
                    self.stt("dve", sg[:, :n], self.x[:, c, t0:t0 + n], g, r[:, :n], ALU.mult, ALU.mult,
                             R=[self.xb[t], rB, self.cb], W=[sgB])
                    self.dma("sp", out_f32_dma[c * 128:(c + 1) * 128, t0:t0 + n], sg[:, :n], sgB.name, R=[sgB])
        self.S.barrier()
        self.sc_off = m0

    def proj_fm(self, W, WB, src, srcB, consumer, banks=(0, 1, 2, 3), tts=range(5)):
        k = 0
        for t in tts:
            t0, n = TT[t]
            for oc in range(8):
                bk = banks[k % len(banks)]
                k += 1
                for kc in range(8):
                    self.mm(self.bank(bk, P, n), W[:, kc, oc * 128:(oc + 1) * 128], src[:, kc, t0:t0 + n], kc == 0, kc == 7,
                            R=[WB, srcB[t]], W=[self.pb[bk]])
                consumer(t, oc, bk)

    def mlp(self, i):
        srcs = []
        for g in range(4):
            srcs.append(self.unit_src("mlp_w1", (0, 1024), (g * 1024, (g + 1) * 1024), i))
            srcs.append(self.unit_src("mlp_w2", (g * 1024, (g + 1) * 1024), (0, 1024), i))
        self.units_begin(srcs)
        self.rmsnorm(PP_NORM + (2 * i + 1) * 8)
        m0 = self.sc_off
        rr = [self.salloc([P, 512], F32, "relu") for _ in range(3)]
        h1 = [self.view(B0 + 8192 * k, [P, 8, 512], BF16) for k in range(2)]
        h1B = [Buf("h1a"), Buf("h1b")]
        kk = [0, 0, 0]

        def g1(W1, W1B, t):
            t0, n = TT[t]
            hb, hB = h1[t % 2], h1B[t % 2]
            for oc in range(8):
                bk = kk[0] % 4
                kk[0] += 1
                for kc in range(8):
                    self.mm(self.bank(bk, P, n), W1[:, kc, oc * 128:(oc + 1) * 128], self.bufA[:, kc, t0:t0 + n], kc == 0, kc == 7,
                            R=[W1B, self.ab[t]], W=[self.pb[bk]])
                r, rB = rr[kk[2] % 3]
                kk[2] += 1
                self.act(r[:, :n], self.bank(bk, P, n), AF.Relu, R=[self.pb[bk], self.cb], W=[rB], bias=self.zero_b, scale=1.0)
                self.tt("pool", hb[:, oc, :n], r[:, :n], r[:, :n], ALU.mult, R=[rB], PW=[hB])

        def g2(W2, W2B, t):
            t0, n = TT[t]
            hb, hB = h1[t % 2], h1B[t % 2]
            for oc in range(8):
                bk = 4 + kk[1] % 4
                kk[1] += 1
                for kc in range(8):
                    self.mm(self.bank(bk, P, n), W2[:, kc, oc * 128:(oc + 1) * 128], hb[:, kc, :n], kc == 0, kc == 7,
                            R=[W2B, hB], W=[self.pb[bk]])
                xs = self.x[:, oc, t0:t0 + n]
                self.tt("dve", xs, xs, self.bank(bk, P, n), ALU.add, R=[self.pb[bk]], W=[self.xb[t]])

        for g in range(4):
            W1, W1B = self.unit_use()
            W2, W2B = self.unit_use()
            g1(W1, W1B, 0)
            for t in range(5):
                if t + 1 < 5:
                    g1(W1, W1B, t + 1)
                if t + 1 == 4:
                    self.unit_done()
                g2(W2, W2B, t)
            self.unit_done()
        self.S.barrier()
        self.sc_off = m0

    def bcast_mid(self, ap2d, nmid):
        apl = [list(a) for a in ap2d.ap]
        return bass.AP(tensor=ap2d.tensor, offset=ap2d.offset, ap=[apl[0], [0, nmid], apl[-1]])

    def allgather(self, name):
        rg = [[0, 1], [2, 3], [4, 5], [6, 7]]
        self.S.add("pool", "collective_compute", ("AllGather", ALU.bypass),
                   dict(replica_groups=rg, ins=[self.D[name + "_in"].ap().opt()], outs=[self.D[name + "_out"].ap().opt()]),
                   R=[self.DB[name + "_in"]], W=[self.DB[name + "_out"]], stream="cc_" + name, inc=1)

    def cumsum32(self, lf, lfB, banks, C=None, CB=None):
        bA, bB = banks
        cs, csB = self.salloc([P, 32, 16], F32, "cs")
        pre, preB = self.salloc([P, 32, 16], F32, "pre")
        if C is None:
            C, CB = self.salloc([P, 32, 16], F32, "C")
        for kt in range(32):
            self.mm(self.bank(bA)[:, kt * 16:(kt + 1) * 16], self.ones_f, lf[:, kt, :], True, True, R=[lfB, self.cb], W=[self.pb[bA]])
        for kt in range(32):
            self.mm(self.bank(bB)[:, kt * 16:(kt + 1) * 16], self.tri_f, lf[:, kt, :], True, True, R=[lfB, self.cb], W=[self.pb[bB]])
        self.cp("dve", cs, self.bank(bA).rearrange("p (a b) -> p a b", b=16), R=[self.pb[bA]], W=[csB])
        self.memset("dve", pre[:, 0, :], 0.0, W=[preB])
        for kt in range(1, 32):
            self.tt("dve", pre[:, kt, :], pre[:, kt - 1, :], cs[:, kt - 1, :], ALU.add, R=[csB, preB], W=[preB])
        self.tt("dve", C, self.bank(bB).rearrange("p (a b) -> p a b", b=16), pre, ALU.add, R=[self.pb[bB], preB], W=[CB])
        return C, CB

    def attn_layer(self, i, fox, j):
        win, widx = ("fox_w_in", None) if fox else ("diff_w_in", j)
        self.units_begin([self.unit_src(win, (0, 1024), (1024, 2048), widx),
                          self.unit_src(win, (0, 1024), (2048, 3072), widx),
                          self.unit_src(win, (0, 1024), (0, 1024), widx)])
        self.rmsnorm(PP_NORM + (2 * i) * 8)
        self.stop("n%d" % i)
        m_layer = self.sc_off
        KsT, KsTB = self.salloc([P, 8, 64], BF16, "KsT")
        Vs = [self.salloc([32, 1024], BF16, "Vs") for _ in range(2)]
        lfn = [self.salloc([32, 16], F32, "lfn") for _ in range(2)] if fox else None
        self.stop("u%d" % i)
        m1 = self.sc_off
        stgf = [self.salloc([P, 512], F32, "stgf%d" % k_) for k_ in range(3)]
        stgb = [self.salloc([P, 512], BF16, "stgb%d" % k_) for k_ in range(3)]
        kout = self.d("fkT") if fox else self.d("dkT")[j]
        vout = self.d("fv") if fox else self.d("dv")[j]
        cnt = [0]
        W, WB = self.unit_use()

        def consK(t, oc, bk):
            t0, n = TT[t]
            sf, sfB = stgf[cnt[0] % 3]
            sb, sbB = stgb[cnt[0] % 3]
            cnt[0] += 1
            import os
            koff = os.environ.get("KOFF", "")
            if "act" not in koff:
                self.cp("act", sf[:, :n], self.bank(bk, P, n), R=[self.pb[bk]], W=[sfB])
            if "kout" not in koff:
                self.dma("sp", kout[oc * 128:(oc + 1) * 128, t0:t0 + n], sf[:, :n], sfB.name, R=[sfB])
            if t < 4:
                if "dve" not in koff:
                    self.cp("dve", sb[:, :n], sf[:, :n], R=[sfB], W=[sbB])
                if "kxin" not in koff:
                    kxn = "kx%s_in" % "ab"[oc // 4]
                    self.dma("sp", self.d(kxn)[(oc % 4) * 128:(oc % 4 + 1) * 128, t0:t0 + n], sb[:, :n], sbB.name, R=[sbB], PW=[self.DB[kxn]])
            elif "kst" not in koff:
                self.cp("dve", KsT[:, oc, :], sf[:, :64], R=[sfB], PW=[KsTB])
        self.proj_fm(W, WB, self.bufA, self.ab, consK)
        self.unit_done()
        self.stop("k%d" % i)
        self.allgather("kxa")
        self.allgather("kxb")
        self.stop("agk%d" % i)
        W, WB = self.unit_use()
        tiles = [(a * 128, 128, a // 4, None) for a in range(16)] + [(2048 + 32 * s, 32, 4, s) for s in range(2)]
        if fox:
            zf = [self.salloc([P, 16], F32, "zf") for _ in range(2)]
            lfst = [self.salloc([P, 16], F32, "lfst%d" % k_) for k_ in range(3)]
        bkc = 0
        for ti, (t0, m, t, s) in enumerate(tiles):
            for half in range(2):
                bk = bkc % 4
                bkc += 1
                for kc in range(8):
                    self.mm(self.bank(bk, m, 512), self.bufA[:, kc, t0:t0 + m], W[:, kc, half * 512:(half + 1) * 512], kc == 0, kc == 7,
                            R=[self.ab[t], WB], W=[self.pb[bk]])
                sf, sfB = stgf[cnt[0] % 3]
                sb, sbB = stgb[cnt[0] % 3]
                cnt[0] += 1
                self.cp("act", sf[0:m, :], self.bank(bk, m, 512), R=[self.pb[bk]], W=[sfB])
                self.dma("sp", vout[t0:t0 + m, half * 512:(half + 1) * 512], sf[0:m, :], sfB.name, R=[sfB])
                if s is None:
                    self.cp("dve", sb[0:m, :], sf[0:m, :], R=[sfB], W=[sbB])
                    vxn = "vx%s_in" % "ab"[t0 // 1024]
                    self.dma("sp", self.d(vxn)[t0 % 1024:t0 % 1024 + m, half * 512:(half + 1) * 512], sb[0:m, :], sbB.name, R=[sbB], PW=[self.DB[vxn]])
                else:
                    self.cp("dve", Vs[s][0][0:32, half * 512:(half + 1) * 512], sf[0:32, :], R=[sfB], PW=[Vs[s][1]])
            if fox:
                bk = bkc % 4
                bkc += 1
                for kc in range(8):
                    self.mm(self.bank(bk, m, 16), self.bufA[:, kc, t0:t0 + m], self.wlf[:, kc, :], kc == 0, kc == 7,
                            R=[self.ab[t], self.cb], W=[self.pb[bk]])
                z, zB = zf[ti % 2]
                self.tt("dve", z[0:m, :], self.bank(bk, m, 16), self.bfb[0:m, :], ALU.add, R=[self.pb[bk], self.cb], W=[zB])
                self.act(z[0:m, :], z[0:m, :], AF.Exp, R=[zB, self.cb], W=[zB], bias=self.zero_b[0:m, :], scale=-1.0)
                self.act(z[0:m, :], z[0:m, :], AF.Ln, R=[zB, self.cb], W=[zB], bias=self.ones_f[0:m, 0:1], scale=1.0)
                lo, loB = lfst[ti % 3]
                self.ts("dve", lo[0:m, :], z[0:m, :], -1.0, None, ALU.mult, None, R=[zB], W=[loB])
                self.dma("sp", self.d("flf")[t0:t0 + m, :], lo[0:m, :], loB.name, R=[loB])
                if s is None:
                    self.dma("sp", self.d("lfx_in")[t0:t0 + m, :], lo[0:m, :], loB.name, R=[loB], PW=[self.DB["lfx_in"]])
                else:
                    self.cp("dve", lfn[s][0], lo[0:32, :], R=[loB], W=[lfn[s][1]])
        self.unit_done()
        self.stop("v%d" % i)
        self.allgather("vxa")
        self.allgather("vxb")
        if fox:
            self.allgather("lfx")
        self.stop("agv%d" % i)
        W, WB = self.unit_use()
        QT = self.view(B0, [P, 8, NTOK], BF16)
        qB = [Buf("q%d" % t) for t in range(5)]

        def consQ(t, oc, bk):
            t0, n = TT[t]
            self.cp("act" if oc % 2 else "dve", QT[:, oc, t0:t0 + n], self.bank(bk, P, n), R=[self.pb[bk]], PW=[qB[t]])
        self.proj_fm(W, WB, self.bufA, self.ab, consQ)
        self.S.barrier()
        self.sc_off = m1
        OT = self.bufA
        oB = [Buf("o%d" % t) for t in range(5)]
        self.stop("q%d" % i)
        self.sample_attn(fox, j, QT, qB, OT, oB, KsT, KsTB, Vs, lfn)
        self.S.barrier()
        self.sc_off = m1
        self.stop("samp%d" % i)
        self.prompt_attn(fox, j, QT, qB, OT, oB)
        self.S.barrier()
        self.stop("prm%d" % i)
        self.sc_off = m_layer
        self.units_begin([self.unit_src("fox_w_out" if fox else "diff_w_out", (0, 1024), (0, 1024), None if fox else j)])
        W, WB = self.unit_use()

        def consO(t, oc, bk):
            t0, n = TT[t]
            xs = self.x[:, oc, t0:t0 + n]
            self.tt("dve", xs, xs, self.bank(bk, P, n), ALU.add, R=[self.pb[bk]], W=[self.xb[t]])
        self.proj_fm(W, WB, OT, oB, consO)
        self.S.barrier()

    def run_pipeline(self, qk, pv, look=2):
        n = len(qk)
        for k in range(n + look):
            if k < n:
                qk[k]()
            if k - look >= 0:
                pv[k - look]()

    def sample_attn(self, fox, j, QT, qB, OT, oB, KsT, KsTB, Vs, lfn):
        ck = [self.view(W0 + 8192 * k, [P, 8, 512], BF16) for k in range(2)]
        cv = [self.view(W0 + 16384 + 8192 * k, [P, 4, 1024], BF16) for k in range(2)]
        ckB = [Buf("ck0"), Buf("ck1")]
        cvB = [Buf("cv0"), Buf("cv1")]
        Pt = [self.salloc([P, 512], BF16, "Pt") for _ in range(3)]
        E = 64 if fox else 128
        NH = 16 if fox else 8
        gl = 0
        Qp, QpB = self.salloc([P, 8, 2, 64], BF16, "Qpad")
        self.memset("dve", Qp, 0.0, W=[QpB])
        self.cp("dve", Qp[0:64, :, 0, :], QT[0:64, :, 2048:2112], R=[qB[4], QpB], PW=[QpB])
        self.cp("dve", Qp[64:128, :, 1, :], QT[64:128, :, 2048:2112], R=[qB[4], QpB], PW=[QpB])
        for s in range(2):
            ms = self.sc_off
            ob, lb = 2 * (s % 2), 2 * (s % 2) + 1
            qc = slice(2048 + 32 * s, 2048 + 32 * s + 32)
            if fox:
                ckT_src = self.d("ckT_f")[s]
                cv_src = self.d("cv_f")[s]
                lfc, lfcB = self.salloc([P, 32, 16], F32, "lfc")
                self.dma("sp", lfc, self.d("clf")[s].rearrange("(kt p) h -> p kt h", p=128), "lfc", W=[lfcB])
                Cc, CcB = self.cumsum32(lfc, lfcB, (4, 5))
                Tb, TbB = self.salloc([P, 16], F32, "Tb")
                self.mm(self.bank(6)[:, 0:16], self.sel127, Cc[:, 31, :], True, True, R=[CcB, self.cb], W=[self.pb[6]])
                self.cp("dve", Tb, self.bank(6)[:, 0:16], R=[self.pb[6]], W=[TbB])
                bias_c, bcB = self.salloc([P, 32, 16], F32, "bias_c")
                self.tt("dve", bias_c, self.bcast_mid(Tb, 32), Cc, ALU.subtract, R=[TbB, CcB], W=[bcB])
                bias_n, bnB = self.salloc([32, 16], F32, "bias_n")
                self.mm(self.bank(7, 32, 16), self.tri_f[0:32, 0:32], lfn[s][0], True, True, R=[lfn[s][1], self.cb], W=[self.pb[7]])
                self.ts("dve", bias_n, self.bank(7, 32, 16), -1.0, None, ALU.mult, None, R=[self.pb[7]], W=[bnB])
            else:
                ckT_src = self.d("ckT_d")[j, s]
                cv_src = self.d("cv_d")[j, s]
            qk, pv = [], []

            def load_group(g):
                sl = (gl + g) % 2
                self.dma("pool", ck[sl], ckT_src[:, g * 512:(g + 1) * 512].rearrange("(c p) n -> p c n", p=128), "ck%d" % sl, W=[ckB[sl]])
                self.dma("pool", cv[sl], cv_src[g * 512:(g + 1) * 512, :].rearrange("(kt p) n -> p kt n", p=128), "cv%d" % sl, W=[cvB[sl]])

            def mk(kt, idx):
                g, kt4 = kt // 4, kt % 4
                sl = (gl + g) % 2 if kt < 32 else None
                new = kt == 32
                rows = 32 if new else P
                sb = 4 + idx % 3
                pt, ptB = Pt[idx % 3]

                def f_qk():
                    special = ((not fox and kt >= 31) or (fox and new)) and 'special' not in soff
                    for h in range(NH):
                        for m in range(1 if fox else 2):
                            c_, ro = (h // 2, 64 * (h % 2)) if fox else (h, 64 * m)
                            col = (h if fox else 2 * h + m) * 32
                            if new:
                                lhsT = KsT[:, c_, 32 * s:32 * s + 32]
                                RB = [KsTB, QpB]
                            else:
                                lhsT = ck[sl][:, c_, kt4 * 128:(kt4 + 1) * 128]
                                RB = [ckB[sl], QpB]
                            if special:
                                if new:
                                    brhs = (self.mnew if fox else self.BsN)[:, col:col + 32]
                                    bl = self.ident_bf[0:32, 0:32]
                                else:
                                    brhs = self.Bs31[:, col:col + 32]
                                    bl = self.ident_bf
                                self.S.add("pe", "matmul", (self.bank(sb, rows, 512)[:, col:col + 32],),
                                           dict(lhsT=bl, rhs=brhs, start=True, stop=False), R=[self.cb], W=[self.pb[sb]])
                            self.S.add("pe", "matmul", (self.bank(sb, rows, 512)[:, col:col + 32],),
                                       dict(lhsT=lhsT, rhs=Qp[:, c_, ro // 64, 32 * s:32 * s + 32], start=not special, stop=True),
                                       R=RB, W=[self.pb[sb]])
                    if fox:
                        for h in range(NH):
                            b_ = bias_n[:, h:h + 1] if new else bias_c[:, kt, h:h + 1]
                            self.act(pt[0:rows, h * 32:(h + 1) * 32], self.bank(sb, rows, 512)[:, h * 32:(h + 1) * 32], AF.Exp,
                                     R=[self.pb[sb], bnB if new else bcB], PW=[ptB] if h else (), W=() if h else [ptB], bias=b_, scale=0.125)
                    else:
                        self.act(pt[0:rows, :], self.bank(sb, rows, 512), AF.Exp, R=[self.pb[sb], self.cb], W=[ptB],
                                 bias=self.zero_b[0:rows, :], scale=0.125)

                def f_pv():
                    if kt == 0:
                        self.mm(self.bank(ob, E, 512), self.zeros_bf[:, 0:E], pt, True, False, R=[ptB, self.cb], W=[self.pb[ob]])
                    for h in range(NH):
                        for m in range(1 if fox else 2):
                            col = (h if fox else 2 * h + m) * 32
                            if new:
                                lhsT = Vs[s][0][0:32, h * E:(h + 1) * E]
                                RB = [Vs[s][1], ptB]
                            else:
                                lhsT = cv[sl][:, kt4, h * E:(h + 1) * E]
                                RB = [cvB[sl], ptB]
                            self.S.add("pe", "matmul", (self.bank(ob, E, 512)[:, col:col + 32],),
                                       dict(lhsT=lhsT, rhs=pt[0:rows, col:col + 32], start=False, stop=False, skip_group_check=True),
                                       R=RB, W=[self.pb[ob]])
                    if new:
                        self.mm(self.bank(ob, E, 512), self.zeros_bf[0:rows, 0:E], pt[0:rows, :], False, True, R=[ptB, self.cb], W=[self.pb[ob]])
                    self.mm(self.bank(lb, E, 512), self.ones_bf[0:rows, 0:E], pt[0:rows, :], kt == 0, new, R=[ptB, self.cb], W=[self.pb[lb]])
                return f_qk, f_pv

            import os
            soff = os.environ.get("SOFF", "")
            for kt in range(32 if "new" in soff else 33):
                a, b_ = mk(kt, kt)
                if "pv" in soff:
                    b_ = (lambda: None)
                if "qk" in soff:
                    a = (lambda: None)
                if kt == 0:
                    qk.append((lambda a=a: (load_group(0), a())))
                elif kt < 28 and kt % 4 == 2:
                    qk.append((lambda g=kt // 4 + 1, a=a: (load_group(g), a())))
                else:
                    qk.append(a)
                pv.append(b_)
            self.run_pipeline(qk, pv)
            gl += 8
            if "fin" in soff:
                self.S.barrier()
                self.sc_off = ms
                continue
            rl, rlB = self.salloc([P, 512], F32, "rl")
            self.recip(rl[0:E, :], self.bank(lb, E, 512), R=[self.pb[lb]], W=[rlB])
            cols = slice(2048 + 32 * s, 2048 + 32 * s + 32)
            if fox:
                o4 = self.bank(ob, 64, 512).rearrange("p (c two q) -> p c two q", two=2, q=32)
                r4 = rl[0:64, :].rearrange("p (c two q) -> p c two q", two=2, q=32)
                self.tt("dve", OT[0:64, :, cols], o4[:, :, 0, :], r4[:, :, 0, :], ALU.mult, R=[self.pb[ob], rlB], PW=[oB[4]])
                so, soB = self.salloc([64, 8, 32], BF16, "so")
                self.tt("dve", so, o4[:, :, 1, :], r4[:, :, 1, :], ALU.mult, R=[self.pb[ob], rlB], W=[soB])
                self.dma("sp", OT[64:128, :, cols], so, soB.name, R=[soB], PW=[oB[4]])
            else:
                of, ofB = self.salloc([P, 512], F32, "of")
                self.tt("dve", of, self.bank(ob), rl, ALU.mult, R=[self.pb[ob], rlB], W=[ofB])
                o4 = of.rearrange("p (h m q) -> p h m q", m=2, q=32)
                oc, ocB = self.salloc([P, 8, 32], F32, "oc")
                self.stt("dve", oc, o4[:, :, 1, :], self.neglam[:, j:j + 1], o4[:, :, 0, :], ALU.mult, ALU.add, R=[ofB, self.cb], W=[ocB])
                sq, sqB = self.salloc([P, 256], F32, "sq")
                ocf = oc.rearrange("p h q -> p (h q)")
                self.tt("dve", sq, ocf, ocf, ALU.mult, R=[ocB], W=[sqB])
                self.mm(self.bank(7, P, 256), self.ones_f, sq, True, True, R=[sqB, self.cb], W=[self.pb[7]])
                ln, lnB = self.salloc([P, 256], F32, "ln")
                self.act(ln, self.bank(7, P, 256), AF.Ln, R=[self.pb[7], self.cb], W=[lnB], bias=self.eps_b, scale=1.0 / 128)
                self.act(ln, ln, AF.Exp, R=[lnB, self.cb], W=[lnB], bias=self.zero_b, scale=-0.5)
                self.stt("dve", OT[:, :, cols], oc, self.gsc[:, j:j + 1], ln.rearrange("p (h q) -> p h q", q=32), ALU.mult, ALU.mult,
                         R=[ocB, lnB, self.cb], PW=[oB[4]])
            self.S.barrier()
            self.sc_off = ms

    def prompt_attn(self, fox, j, QT, qB, OT, oB):
        KT = [self.view(W0 + 8192 * k, [P, 4096], BF16) for k in range(2)]
        VC = [self.view(W0 + 16384 + 8192 * k, [P, 32, 128], BF16) for k in range(2)]
        KTB = [Buf("KT0"), Buf("KT1")]
        VCB = [Buf("VC0"), Buf("VC1")]
        Pt = [self.salloc([P, 512], BF16, "Pt") for _ in range(3 if fox else 4)]
        rl, rlB = self.salloc([P, 512], F32, "rl")
        E = 64 if fox else 128
        if fox:
            strip = [self.salloc([P, SW], BF16, "strip")]
            self.dma("pool", strip[0][0], self.d("mstrip_f"), "stripf", W=[strip[0][1]])
            stgo = [self.salloc([64, 512], BF16, "stgo%d" % k_) for k_ in range(1)]
            biasq = [self.salloc([P, 32, 16], F32, "biasq%d" % k_) for k_ in range(4)]
            Cg, CgB = self.salloc([P, 32, 16], F32, "Cg")
            refs, refsB = self.salloc([P, 4, 2, 16], F32, "refs")
            ref, refB = self.salloc([P, 4, 16], F32, "ref")
            mlf = self.sc_off
            LF, LFB = self.salloc([P, 32, 16], F32, "LF")
            for g in range(8):
                r, blk = g % 2, g // 2
                src = self.d("lfx_out")[r * 2048 + blk * 512:r * 2048 + (blk + 1) * 512, :].rearrange("(kt p) h -> p kt h", p=128)
                self.dma("sp", LF[:, g * 4:(g + 1) * 4, :], src, "LF", R=[self.DB["lfx_out"]], PW=[LFB] if g else (), W=() if g else [LFB])
            self.cumsum32(LF, LFB, (4, 5), Cg, CgB)
            for i in range(4):
                for ab in range(2):
                    self.mm(self.bank(6)[:, (2 * i + ab) * 16:(2 * i + ab + 1) * 16], self.sel127, Cg[:, 8 * i + 1 + 4 * ab, :], True, True,
                            R=[CgB, self.cb], W=[self.pb[6]])
            self.cp("dve", refs.rearrange("p a b c -> p (a b c)"), self.bank(6)[:, 0:128], R=[self.pb[6]], W=[refsB])
            self.ts("dve", ref, refs[:, :, 0, :], self.selb[:, 0:1], None, ALU.mult, None, R=[refsB, self.cb], W=[refB])
            self.stt("dve", ref, refs[:, :, 1, :], self.selb[:, 1:2], ref, ALU.mult, ALU.add, R=[refsB, refB, self.cb], W=[refB])
            for i in range(4):
                self.tt("dve", biasq[i][0], self.bcast_mid(ref[:, i, :], 32), Cg, ALU.subtract, R=[refB, CgB], W=[biasq[i][1]])
            self.S.barrier()
            self.sc_off = mlf
        else:
            strip = [self.salloc([P, SW], BF16, "strip") for _ in range(2)]
            om = [self.salloc([P, 512], F32, "om") for _ in range(2)]
            oc, ocB = self.salloc([P, 512], F32, "oc")
            sqo, sqoB = self.salloc([P, 512], F32, "sqo")
            lnv, lnvB = self.salloc([P, 512], F32, "lnv")

        def load_c(c):
            sl = c % 2
            for r in range(2):
                kxo = "kx%s_out" % "ab"[c // 4]
                src = self.d(kxo)[r * 512 + (c % 4) * 128:r * 512 + (c % 4 + 1) * 128, :].rearrange("p (b n) -> p b n", n=512)
                dst = KT[sl].rearrange("p (b r n) -> p b r n", r=2, n=512)[:, :, r, :]
                self.dma("sp", dst, src, "KT%d" % sl, R=[self.DB[kxo]], PW=[KTB[sl]] if r else (), W=() if r else [KTB[sl]])
            for r in range(2):
                for b4 in range(4):
                    vxo = "vx%s_out" % "ab"[b4 // 2]
                    src = self.d(vxo)[r * 1024 + (b4 % 2) * 512:r * 1024 + (b4 % 2 + 1) * 512, c * 128:(c + 1) * 128].rearrange("(s p) e -> p s e", p=128)
                    dst = VC[sl].rearrange("p (b r s) e -> p b r s e", r=2, s=4)[:, b4, r, :, :]
                    first_ = (r == 0 and b4 == 0)
                    self.dma("sp", dst, src, "VC%d" % sl, R=[self.DB[vxo]], PW=() if first_ else [VCB[sl]], W=[VCB[sl]] if first_ else ())
            if not fox:
                self.dma("sp", strip[sl][0], self.D["strips"].ap()[c], "strip%d" % sl, R=[self.DB["strips"]], W=[strip[sl][1]])

        deferred = []
        cnt = [0, 0]
        qk, pv = [], []

        def flush_deferred(force=False):
            while deferred and (force or deferred[0][0] <= cnt[0]):
                deferred.pop(0)[1]()

        def mk(c, u, i, kt, last, grp):
            idx = len(qk)
            sl = c % 2
            if fox:
                h = 2 * c + u
                ro = 64 * u
            else:
                h = c
                ro = 64 * u
            sb = 4 + idx % 3
            pt, ptB = Pt[idx % len(Pt)]
            ob, lb = 2 * (grp % 2), 2 * (grp % 2) + 1
            s_ = kt - (8 * i - 1)
            special = (1 <= s_ <= 8) if fox else (0 <= s_ <= 8)
            st, stB = strip[0] if fox else strip[sl]
            qs = slice(i * 512, (i + 1) * 512)

            def f_qk():
                self.mm(self.bank(sb), KT[sl][ro:ro + 64, kt * 128:(kt + 1) * 128], QT[ro:ro + 64, c, qs], True, not special,
                        R=[KTB[sl], qB[i]], W=[self.pb[sb]])
                if special:
                    w0 = 128 * (8 - s_)
                    self.mm(self.bank(sb), self.ident_bf, st[:, w0:w0 + 512], False, True, R=[stB, self.cb], W=[self.pb[sb]])
                if fox:
                    self.act(pt, self.bank(sb), AF.Exp, R=[self.pb[sb], biasq[i][1]], W=[ptB], bias=biasq[i][0][:, kt, h:h + 1], scale=0.125)
                else:
                    self.act(pt, self.bank(sb), AF.Exp, R=[self.pb[sb], self.cb], W=[ptB], bias=self.zero_b, scale=0.125)

            def f_pv():
                flush_deferred()
                cnt[0] += 1
                lhsT = VC[sl][:, kt, ro:ro + 64] if fox else VC[sl][:, kt, :]
                self.mm(self.bank(ob, E, 512), lhsT, pt, kt == 0, last, R=[VCB[sl], ptB], W=[self.pb[ob]])
                self.mm(self.bank(lb, E, 512), self.ones_bf[:, 0:E], pt, kt == 0, last, R=[ptB, self.cb], W=[self.pb[lb]])
                if not last:
                    return
                self.recip(rl[0:E, :], self.bank(lb, E, 512), R=[self.pb[lb]], W=[rlB])
                if fox:
                    if u == 0:
                        self.tt("dve", OT[0:64, c, qs], self.bank(ob, 64, 512), rl[0:64, :], ALU.mult, R=[self.pb[ob], rlB], PW=[oB[i]])
                    else:
                        so, soB = stgo[0]
                        self.tt("dve", so, self.bank(ob, 64, 512), rl[0:64, :], ALU.mult, R=[self.pb[ob], rlB], W=[soB])
                        self.dma("sp", OT[64:128, c, qs], so, soB.name, R=[soB], PW=[oB[i]])
                    return
                self.tt("dve", om[u][0], self.bank(ob), rl, ALU.mult, R=[self.pb[ob], rlB], W=[om[u][1]])
                if u == 0:
                    return
                self.stt("dve", oc, om[1][0], self.neglam[:, j:j + 1], om[0][0], ALU.mult, ALU.add, R=[om[0][1], om[1][1], self.cb], W=[ocB])
                self.tt("dve", sqo, oc, oc, ALU.mult, R=[ocB], W=[sqoB])

                def fin():
                    self.mm(self.bank(7), self.ones_f, sqo, True, True, R=[sqoB, self.cb], W=[self.pb[7]])
                    self.act(lnv, self.bank(7), AF.Ln, R=[self.pb[7], self.cb], W=[lnvB], bias=self.eps_b, scale=1.0 / 128)
                    self.act(lnv, lnv, AF.Exp, R=[lnvB, self.cb], W=[lnvB], bias=self.zero_b, scale=-0.5)
                    self.stt("dve", OT[:, c, qs], oc, self.gsc[:, j:j + 1], lnv, ALU.mult, ALU.mult, R=[ocB, lnvB, self.cb], PW=[oB[i]])
                deferred.append((cnt[0] + 4, fin))
            return f_qk, f_pv

        for c in range(8):
            first = len(qk)
            for i in range(4):
                for u in range(2):
                    nk = 8 * i + 8
                    for kt in range(nk):
                        a, b_ = mk(c, u, i, kt, kt == nk - 1, cnt[1])
                        qk.append(a)
                        pv.append(b_)
                    cnt[1] += 1
            if c == 0:
                a0 = qk[first]
                qk[first] = (lambda a0=a0: (load_c(0), load_c(1), a0()))
            elif c + 1 < 8:
                a0 = qk[first + 2]
                qk[first + 2] = (lambda a0=a0, c=c: (load_c(c + 1), a0()))
        self.run_pipeline(qk, pv)
        flush_deferred(force=True)

    def conv_layer(self, i):
        self.rmsnorm(PP_NORM + (2 * i) * 8)
        m0 = self.sc_off
        UW = 2304
        ub = self.view(B0, [P, 8, UW], BF16)
        ubB = [Buf("u%d" % t) for t in range(5)]
        SEG = 542
        SB0 = 4 * SEG

        def useg(t, c, lo, n):
            if t < 4:
                return ub[:, c, t * SEG + 30 + lo:t * SEG + 30 + lo + n]
            return None
        self.units_begin([self.unit_src("conv_w_pw1", (0, 1024), (0, 1024)), self.unit_src("conv_w_pw1", (0, 1024), (1024, 2048)),
                          self.unit_src("conv_w_pw2", (0, 1024), (0, 1024))])
        WA, WAB = self.unit_use()
        WG, WGB = self.unit_use()
        uf = [self.salloc([P, 512], F32, "uf%d" % k_) for k_ in range(2)]
        eg = [self.salloc([P, 512], F32, "eg") for _ in range(2)]
        for s in range(2):
            self.dma("pool", ub[:, :, SB0 + 62 * s:SB0 + 62 * s + 30], self.d("scT")[s].rearrange("(c p) n -> p c n", p=128), "scst",
                     PW=[ubB[4]])
        k = 0
        for t, (t0, n) in enumerate(TT):
            for oc in range(8):
                ba, bg = (k % 2) * 2, (k % 2) * 2 + 1
                for kc in range(8):
                    self.mm(self.bank(ba, P, n), WA[:, kc, oc * 128:(oc + 1) * 128], self.bufA[:, kc, t0:t0 + n], kc == 0, kc == 7,
                            R=[WAB, self.ab[t]], W=[self.pb[ba]])
                for kc in range(8):
                    self.mm(self.bank(bg, P, n), WG[:, kc, oc * 128:(oc + 1) * 128], self.bufA[:, kc, t0:t0 + n], kc == 0, kc == 7,
                            R=[WGB, self.ab[t]], W=[self.pb[bg]])
                e, eB = eg[k % 2]
                u, uB = uf[k % 2]
                k += 1
                self.act(e[:, :n], self.bank(bg, P, n), AF.Exp, R=[self.pb[bg], self.cb], W=[eB], bias=self.negbg[:, oc:oc + 1], scale=-1.0)
                self.ts("dve", e[:, :n], e[:, :n], 1.0, None, ALU.add, None, R=[eB], W=[eB])
                self.recip(e[:, :n], e[:, :n], R=[eB], W=[eB])
                self.stt("dve", u[:, :n], self.bank(ba, P, n), self.pp[:, PP_BPW1 + oc:PP_BPW1 + oc + 1], e[:, :n], ALU.add, ALU.mult,
                         R=[self.pb[ba], eB, self.cb], W=[uB])
                if t < 4:
                    self.cp("act", useg(t, oc, 0, 512), u[:, :512], R=[uB], PW=[ubB[t]])
                    if t == 3:
                        self.dma("sp", self.d("cvT")[oc * 128:(oc + 1) * 128, 0:30], u[:, 482:512], uB.name, R=[uB])
                else:
                    dst = ub[:, oc, SB0:SB0 + 124].rearrange("p (s n) -> p s n", n=62)[:, :, 30:62]
                    self.cp("act", dst, u[:, 0:64].rearrange("p (s n) -> p s n", n=32), R=[uB], PW=[ubB[4]])
                    self.dma("sp", self.d("cvT")[oc * 128:(oc + 1) * 128, 30:94], u[:, 0:64], uB.name, R=[uB])
        self.unit_done()
        self.unit_done()
        cx_in, cx_out = self.d("cx_in"), self.d("cx_out")
        for bi in range(4):
            self.dma("sp", cx_in[bi * 1024:(bi + 1) * 1024, 0:32].rearrange("(c p) n -> p c n", p=128),
                     ub[:, :, bi * SEG + 510:bi * SEG + 542], "cxw", R=[ubB[bi]], PW=[self.DB["cx_in"]])
        self.allgather("cx")
        self.S.barrier()
        self.sc_off = m0
        hA, hAB = self.salloc([P, 8, 30], BF16, "hA")
        hB, hBB = self.salloc([P, 8, 30], BF16, "hB")
        tmp, tmpB = self.salloc([P, 8, 30], F32, "htmp")
        for bi in range(4):
            ia = max(bi - 1, 0)
            self.dma("sp", hA, cx_out[4096 + ia * 1024:4096 + (ia + 1) * 1024, 2:32].rearrange("(c p) n -> p c n", p=128), "hA",
                     R=[self.DB["cx_out"]], W=[hAB])
            self.dma("sp", hB, cx_out[bi * 1024:(bi + 1) * 1024, 2:32].rearrange("(c p) n -> p c n", p=128), "hB",
                     R=[self.DB["cx_out"]], W=[hBB])
            self.ts("dve", tmp, hA, self.selb[:, 2 + 2 * bi:3 + 2 * bi], None, ALU.mult, None, R=[hAB, self.cb], W=[tmpB])
            self.stt("dve", ub[:, :, bi * SEG:bi * SEG + 30], hB, self.selb[:, 3 + 2 * bi:4 + 2 * bi], tmp, ALU.mult, ALU.add,
                     R=[hBB, tmpB, self.cb], PW=[ubB[bi]])
        W2, W2B = self.unit_use()
        ybf = self.view(A0, [P, 8, 512], BF16)
        zbf = self.view(A0 + 8192, [P, 8, 512], BF16)
        ybB, zbB = Buf("ybf"), Buf("zbf")
        acc = {"dve": [self.salloc([P, 512], F32, "accd") for _ in range(2)]}
        ysq = {"dve": self.salloc([P, 512], F32, "ysqd")}
        dg = [(self.view(A0 + 16384 + 8192 * k_, [P, 31, 128], BF16), Buf("dg%d" % k_)) for k_ in range(2)]
        mean, meanB = self.salloc([P, 512], F32, "mean")
        var, varB = self.salloc([P, 512], F32, "var")
        rs, rsB = self.salloc([P, 512], F32, "rs")
        t1 = [self.salloc([P, 512], F32, "t1") for _ in range(2)]
        ez = [self.salloc([P, 512], F32, "ez") for _ in range(2)]
        kk = 0
        for t, (t0, n) in enumerate(TT):
            for c in range(8):
                eng = "dve"
                a, aB = acc[eng][c % 2]
                if t < 4:
                    def src(jj):
                        return ub[:, c, t * SEG + jj:t * SEG + jj + 512]
                    av = a
                else:
                    def src(jj):
                        return ub[:, c, SB0:SB0 + 124].rearrange("p (s n) -> p s n", n=62)[:, :, jj:jj + 32]
                    av = a[:, 0:64].rearrange("p (s n) -> p s n", n=32)
                w = self.pp[:, PP_WDW + c * 31:PP_WDW + c * 31 + 31]
                if t < 4:
                    dgk = (t * 8 + c) % 2
                    dgv, dgB = dg[dgk]
                    for jj in range(31):
                        self.ts("dve", dgv[:, jj, :], self.ident_bf, w[:, jj:jj + 1], None, ALU.mult, None, R=[self.cb],
                                W=[dgB] if jj == 0 else (), PW=() if jj == 0 else [dgB])
                    cbk = 6 + dgk
                    for jj in range(31):
                        self.mm(self.bank(cbk), dgv[:, jj, :], src(jj), jj == 0, jj == 30, R=[dgB, ubB[t]], W=[self.pb[cbk]])
                    self.ts("dve", av, self.bank(cbk), self.pp[:, PP_BDW + c:PP_BDW + c + 1], None, ALU.add, None,
                            R=[self.pb[cbk], self.cb], W=[aB])
                else:
                    self.ts(eng, av, src(0), w[:, 0:1], self.pp[:, PP_BDW + c:PP_BDW + c + 1], ALU.mult, ALU.add, R=[ubB[t], self.cb], W=[aB])
                    for jj in range(1, 31):
                        self.stt(eng, av, src(jj), w[:, jj:jj + 1], av, ALU.mult, ALU.add, R=[ubB[t], self.cb, aB], W=[aB])
                q, qB_ = ysq[eng]
                self.tt(eng, q[:, :n], a[:, :n], a[:, :n], ALU.mult, R=[aB], W=[qB_])
                self.mm(self.bank(0, P, n), self.ones_f, a[:, :n], c == 0, c == 7, R=[aB, self.cb], W=[self.pb[0]])
                self.mm(self.bank(1, P, n), self.ones_f, q[:, :n], c == 0, c == 7, R=[qB_, self.cb], W=[self.pb[1]])
                self.cp("act", ybf[:, c, :n], a[:, :n], R=[aB], PW=[ybB])
            self.ts("dve", mean[:, :n], self.bank(0, P, n), 1.0 / 1024, None, ALU.mult, None, R=[self.pb[0]], W=[meanB])
            self.tt("dve", var[:, :n], mean[:, :n], mean[:, :n], ALU.mult, R=[meanB], W=[varB])
            self.stt("dve", var[:, :n], self.bank(1, P, n), 1.0 / 1024, var[:, :n], ALU.mult, ALU.subtract, R=[self.pb[1], varB], W=[varB])
            self.act(rs[:, :n], var[:, :n], AF.Ln, R=[varB, self.cb], W=[rsB], bias=self.eps_b, scale=1.0)
            self.act(rs[:, :n], rs[:, :n], AF.Exp, R=[rsB, self.cb], W=[rsB], bias=self.zero_b, scale=-0.5)
            for c in range(8):
                a, aB = t1[c % 2]
                e, eB = ez[c % 2]
                self.tt("dve", a[:, :n], ybf[:, c, :n], mean[:, :n], ALU.subtract, R=[ybB, meanB], W=[aB])
                self.stt("dve", a[:, :n], a[:, :n], self.pp[:, PP_LNG + c:PP_LNG + c + 1], rs[:, :n], ALU.mult, ALU.mult, R=[aB, rsB, self.cb], W=[aB])
                self.act(e[:, :n], a[:, :n], AF.Exp, R=[aB, self.cb], W=[eB], bias=self.neglnb[:, c:c + 1], scale=-1.0)
                self.ts("pool", e[:, :n], e[:, :n], 1.0, None, ALU.add, None, R=[eB], W=[eB])
                self.recip(e[:, :n], e[:, :n], R=[eB], W=[eB])
                self.stt("dve", zbf[:, c, :n], a[:, :n], self.pp[:, PP_LNB + c:PP_LNB + c + 1], e[:, :n], ALU.add, ALU.mult,
                         R=[aB, eB, self.cb], PW=[zbB])
            for oc in range(8):
                bk = 2 + kk % 4
                kk += 1
                for kc in range(8):
                    self.mm(self.bank(bk, P, n), W2[:, kc, oc * 128:(oc + 1) * 128], zbf[:, kc, :n], kc == 0, kc == 7, R=[W2B, zbB], W=[self.pb[bk]])
                xs = self.x[:, oc, t0:t0 + n]
                self.stt("dve", xs, self.bank(bk, P, n), self.pp[:, PP_BPW2 + oc:PP_BPW2 + oc + 1], xs, ALU.add, ALU.add,
                         R=[self.pb[bk], self.cb], W=[self.xb[t]])
        self.S.barrier()
        self.sc_off = m0

    def build(self):
        self.declare()
        try:
            self.prologue()
            self.stop("pro")
            for i in range(DEPTH):
                kind, j = i % 3, i // 3
                if kind == 0:
                    self.attn_layer(i, False, j)
                elif kind == 1:
                    self.conv_layer(i)
                else:
                    self.attn_layer(i, True, j)
                self.stop("mix%d" % i)
                self.mlp(i)
                self.stop("mlp%d" % i)
            self.rmsnorm(PP_FINAL, out_f32_dma=self.d("yT"))
        except StopBuild:
            import os
            if os.environ.get("KDUMP") == "2":
                self.S.barrier()
                for cch in range(8):
                    self.dma("pool", self.d("yT")[cch * 128:(cch + 1) * 128, :], self.bufA[:, cch, :], "xdump", R=self.xb)
            elif os.environ.get("KDUMP"):
                self.S.barrier()
                for cch in range(8):
                    self.dma("sp", self.d("yT")[cch * 128:(cch + 1) * 128, :], self.x[:, cch, :], "xdump", R=self.xb)
        self.S.finalize()
        nc, S = self.nc, self.S
        with nc.Block() as block:
            @block.tensor
            def _(e):
                S.emit("pe", e)

            @block.scalar
            def _(e):
                S.emit("act", e)

            @block.vector
            def _(e):
                S.emit("dve", e)

            @block.gpsimd
            def _(e):
                S.emit("pool", e, final_wait=True)

            @block.sync
            def _(e):
                S.emit("sp", e, final_wait=True)


def build_program():
    from contextlib import ExitStack
    nc = bass.Bass("TRN2", target_bir_lowering=False)
    stack = ExitStack()
    with stack:
        pr = Prog(nc, stack)
        pr.build()
    return nc, set(pr.in_shapes) & set(pr.D)


def _t5_bucket_np(n):
    try:
        import jax
        import jax.numpy as jnp
        with jax.default_device(jax.devices("cpu")[0]):
            rel = jnp.asarray(-n, jnp.int32)
            half = 16
            nn = -rel
            offset = jnp.where(nn < 0, half, 0)
            nn = jnp.abs(nn)
            max_exact = half // 2
            large = max_exact + (jnp.log(jnp.maximum(nn, 1).astype(jnp.float32) / max_exact)
                                 / math.log(128 / max_exact) * (half - max_exact)).astype(jnp.int32)
            large = jnp.minimum(large, half - 1)
            return np.asarray(offset + jnp.where(nn < max_exact, nn, large))
    except Exception:
        nn = np.asarray(n, np.int64)
        offset = np.where(nn < 0, 16, 0)
        a = np.abs(nn)
        large = 8 + (np.log(np.maximum(a, 1).astype(np.float32) / np.float32(8)) / np.float32(math.log(16.0)) * np.float32(8)).astype(np.int32)
        large = np.minimum(large, 15)
        return offset + np.where(a < 8, a, large)


def _consts(par):
    c = {}
    xs = np.arange(RZ)
    n1 = xs - 1023 + 512 * par
    b1 = _t5_bucket_np(n1)
    oh1 = np.zeros((32, RZ), np.float32)
    oh1[b1, xs] = 1.0
    c["oh1"] = oh1
    xs2 = np.arange(RZ2)
    b2 = _t5_bucket_np(xs2 - 127)
    oh2 = np.zeros((32, RZ2), np.float32)
    oh2[b2, xs2] = 1.0
    c["oh2"] = oh2
    kl = np.arange(128)[:, None]
    jp = np.arange(SW)[None, :]
    qrel = jp - 896 + 512 * par
    c["mstrip_d"] = np.where((kl // 64) > np.floor_divide(qrel, 64), NEG, 0.0).astype(np.float32)
    c["mstrip_f"] = np.where(kl > qrel, NEG, 0.0).astype(np.float32)
    sel = np.zeros((1, 16), np.float32)
    sel[0, 0] = 1.0 if par == 0 else 0.0
    sel[0, 1] = 1.0 if par == 1 else 0.0
    for bi in range(4):
        sel[0, 2 + 2 * bi] = 1.0 if (par == 0 and bi >= 1) else 0.0
        sel[0, 3 + 2 * bi] = 1.0 if par == 1 else 0.0
    c["selp"] = sel
    k_ = np.arange(128)
    c["tri"] = (k_[:, None] <= k_[None, :]).astype(np.float32)
    c["ident"] = np.eye(128, dtype=np.float32)
    s127 = np.zeros((128, 128), np.float32)
    s127[127, :] = 1.0
    c["sel127"] = s127
    kk = np.arange(32)[:, None]
    qq = np.arange(32)[None, :]
    mn = np.where(kk > qq, NEG, 0.0).astype(np.float32)
    c["masknew"] = np.tile(mn, (1, 16))
    return c


_NC_CACHE = {}


def kernel(x_prompt, x_sample, cache_diff_k, cache_diff_v, state_conv, cache_fox_k, cache_fox_v,
           cache_fox_logf, rel_bias, norm_g, final_g, diff_w_in, diff_w_out, diff_lq1, diff_lk1,
           diff_lq2, diff_lk2, diff_subln_g, conv_w_pw1, conv_b_pw1, conv_w_dw, conv_b_dw,
           conv_ln_g, conv_ln_b, conv_w_pw2, conv_b_pw2, fox_w_in, fox_b_f, fox_w_out,
           mlp_w1, mlp_w2):
    f = lambda a: np.ascontiguousarray(np.asarray(a, dtype=np.float32))
    x_prompt, x_sample = f(x_prompt), f(x_sample)
    pp = np.zeros((128, NPP), np.float32)

    def fm(v):
        return np.asarray(v, np.float32).reshape(8, 128).T
    ng = np.asarray(norm_g, np.float32)
    for i in range(4):
        for k in range(2):
            pp[:, PP_NORM + (2 * i + k) * 8:PP_NORM + (2 * i + k) * 8 + 8] = fm(ng[i, k])
    pp[:, PP_FINAL:PP_FINAL + 8] = fm(final_g)
    pp[:, PP_SUBLN:PP_SUBLN + 2] = np.asarray(diff_subln_g, np.float32).T
    pp[:, PP_BPW1:PP_BPW1 + 16] = np.asarray(conv_b_pw1, np.float32)[0].reshape(16, 128).T
    wdw = np.asarray(conv_w_dw, np.float32)[0]
    pp[:, PP_WDW:PP_WDW + 248] = wdw.T.reshape(8, 128, 31).transpose(1, 0, 2).reshape(128, 248)
    pp[:, PP_BDW:PP_BDW + 8] = fm(np.asarray(conv_b_dw)[0])
    pp[:, PP_LNG:PP_LNG + 8] = fm(np.asarray(conv_ln_g)[0])
    pp[:, PP_LNB:PP_LNB + 8] = fm(np.asarray(conv_ln_b)[0])
    pp[:, PP_BPW2:PP_BPW2 + 8] = fm(np.asarray(conv_b_pw2)[0])
    rowp = np.zeros((1, NROW), np.float32)
    for j in range(2):
        for k, v in enumerate((diff_lq1, diff_lk1, diff_lq2, diff_lk2)):
            rowp[0, j * 256 + k * 64:j * 256 + k * 64 + 64] = np.asarray(v, np.float32)[j]
    rowp[0, 512:528] = np.asarray(fox_b_f, np.float32)[0]
    shared = {
        "diff_w_in": f(diff_w_in), "diff_w_out": f(diff_w_out), "conv_w_pw1": f(conv_w_pw1)[0], "conv_w_pw2": f(conv_w_pw2)[0],
        "fox_w_in": f(fox_w_in)[0], "fox_w_out": f(fox_w_out)[0], "mlp_w1": f(mlp_w1), "mlp_w2": f(mlp_w2),
        "pp": pp, "rowp": rowp, "rel_bias": f(rel_bias),
    }
    consts = [_consts(0), _consts(1)]
    cdk = np.asarray(cache_diff_k, np.float32).reshape(2, 16, 4096, 1024)
    cdv = np.asarray(cache_diff_v, np.float32).reshape(2, 16, 4096, 1024)
    cfk = np.asarray(cache_fox_k, np.float32).reshape(16, 4096, 1024)
    cfv = np.asarray(cache_fox_v, np.float32).reshape(16, 4096, 1024)
    cfl = np.asarray(cache_fox_logf, np.float32).reshape(16, 4096, 16)
    stc = np.asarray(state_conv, np.float32)[0]
    in_maps = []
    for c in range(8):
        b, par = c // 2, c % 2
        toks = np.concatenate([np.arange(512 * (2 * i + par), 512 * (2 * i + par) + 512) for i in range(4)])
        xs = np.concatenate([x_prompt[b][toks], x_sample[2 * c], x_sample[2 * c + 1]], 0)
        m = dict(shared)
        m.update(consts[par])
        m["xT"] = np.ascontiguousarray(xs.T)
        ss = [2 * c, 2 * c + 1]
        m["ckT_d"] = np.ascontiguousarray(cdk[:, ss].transpose(0, 1, 3, 2))
        m["cv_d"] = np.ascontiguousarray(cdv[:, ss])
        m["ckT_f"] = np.ascontiguousarray(cfk[ss].transpose(0, 2, 1))
        m["cv_f"] = np.ascontiguousarray(cfv[ss])
        m["clf"] = np.ascontiguousarray(cfl[ss])
        m["scT"] = np.ascontiguousarray(stc[ss].transpose(0, 2, 1))
        in_maps.append(m)
    if "nc" not in _NC_CACHE:
        _NC_CACHE["nc"] = build_program()
    nc, used = _NC_CACHE["nc"]
    in_maps = [{k: v for k, v in m.items() if k in used} for m in in_maps]
    res = run_bass_kernel_spmd(nc, in_maps, core_ids=list(range(8)))
    R = res.results
    B, S_, D = 4, 4096, 1024
    y_p = np.zeros((B, S_, D), np.float32); y_s = np.zeros((16, 32, D), np.float32)
    dk_p = np.zeros((2, B, S_, 1024), np.float32); dv_p = np.zeros((2, B, S_, 1024), np.float32)
    cv_p = np.zeros((1, B, 30, D), np.float32)
    fk_p = np.zeros((1, B, S_, 1024), np.float32); fv_p = np.zeros((1, B, S_, 1024), np.float32); fl_p = np.zeros((1, B, S_, 16), np.float32)
    dk_s = np.zeros((2, 16, 32, 1024), np.float32); dv_s = np.zeros((2, 16, 32, 1024), np.float32)
    cv_s = np.zeros((1, 16, 30, D), np.float32)
    fk_s = np.zeros((1, 16, 32, 1024), np.float32); fv_s = np.zeros((1, 16, 32, 1024), np.float32); fl_s = np.zeros((1, 16, 32, 16), np.float32)
    for c in range(8):
        b, par = c // 2, c % 2
        toks = np.concatenate([np.arange(512 * (2 * i + par), 512 * (2 * i + par) + 512) for i in range(4)])
        r = R[c]
        yT = r["yT"]
        y_p[b, toks] = yT[:, :2048].T
        dkT, dv, fkT, fv, flf, cvT = r["dkT"], r["dv"], r["fkT"], r["fv"], r["flf"], r["cvT"]
        for j in range(2):
            dk_p[j, b, toks] = dkT[j][:, :2048].T
            dv_p[j, b, toks] = dv[j][:2048]
        fk_p[0, b, toks] = fkT[:, :2048].T
        fv_p[0, b, toks] = fv[:2048]
        fl_p[0, b, toks] = flf[:2048]
        if par == 1:
            cv_p[0, b] = cvT[:, 0:30].T
        for s in range(2):
            st = 2 * c + s
            sl = slice(2048 + 32 * s, 2048 + 32 * s + 32)
            y_s[st] = yT[:, sl].T
            for j in range(2):
                dk_s[j, st] = dkT[j][:, sl].T
                dv_s[j, st] = dv[j][sl]
            fk_s[0, st] = fkT[:, sl].T
            fv_s[0, st] = fv[sl]
            fl_s[0, st] = flf[sl]
            cv_s[0, st] = cvT[:, 30 + 32 * s + 2:30 + 32 * s + 32].T
    return (y_p, y_s,
            dk_p.reshape(2, B, S_, 8, 2, 64), dv_p.reshape(2, B, S_, 8, 128), cv_p,
            fk_p.reshape(1, B, S_, 16, 64), fv_p.reshape(1, B, S_, 16, 64), fl_p,
            dk_s.reshape(2, 16, 32, 8, 2, 64), dv_s.reshape(2, 16, 32, 8, 128), cv_s,
            fk_s.reshape(1, 16, 32, 16, 64), fv_s.reshape(1, 16, 32, 16, 64), fl_s)
```
